# Optimizing a Trainium2 kernel written in Bass

```python
import math
import jax, jax.numpy as jnp
from jax import lax
import numpy as np

D_MODEL = 2048
BATCH = 4
SEQ = 2048
DEPTH = 1

HG_HEADS = 16
HG_KDIM = 128
HG_VDIM = 128
HG_WIDTH = HG_HEADS * HG_KDIM
HG_CHUNK = 64
ATTN_GROUPS = ((128, 1), (512, 4), (2048, 16))
ATTN_HEADS_PER_GROUP = 4
HEAD_DIM = 128
ATTN_WIDTH = len(ATTN_GROUPS) * ATTN_HEADS_PER_GROUP * HEAD_DIM
ATTN_OUT = ATTN_HEADS_PER_GROUP * HEAD_DIM
ROPE_THETA = 10000.0
D_FF = 5632
CONV_WIDTH = 3
NORM_EPS = 1e-6
NEG_INF = -1e30
IN_COLS = 4 * HG_WIDTH + 3 * ATTN_WIDTH + 2 * D_MODEL

kernel_name = "hybrid_hgrn2_dilated_attn_convffn"


def rmsnorm(x, g):
    x32 = x.astype(jnp.float32)
    y = x32 * lax.rsqrt(jnp.mean(x32 * x32, axis=-1, keepdims=True) + NORM_EPS)
    return (y * g.astype(jnp.float32)).astype(x.dtype)


def rope(x):
    T, Dh = x.shape[1], x.shape[-1]
    half = Dh // 2
    inv_freq = jnp.exp(-math.log(ROPE_THETA) * jnp.arange(half, dtype=jnp.float32) * (2.0 / Dh))
    ang = jnp.arange(T, dtype=jnp.float32)[:, None] * inv_freq[None, :]
    cos = jnp.cos(ang)[None, :, None, :]
    sin = jnp.sin(ang)[None, :, None, :]
    x32 = x.astype(jnp.float32)
    x1, x2 = x32[..., :half], x32[..., half:]
    return jnp.concatenate([x1 * cos - x2 * sin, x2 * cos + x1 * sin], axis=-1).astype(x.dtype)


def hgrn2_branch(q_raw, f_raw, i_raw, g_raw, lb, norm_g):
    B, T, _ = q_raw.shape
    H, K, V, C = HG_HEADS, HG_KDIM, HG_VDIM, HG_CHUNK
    N = T // C
    f32 = jnp.float32
    z = f_raw.astype(f32)
    lb = lb.astype(f32)
    log_f = jnp.logaddexp(jnp.log(lb), jnp.log1p(-lb) + jax.nn.log_sigmoid(z))
    k = (1.0 - lb) * jax.nn.sigmoid(-z)
    q = jax.nn.silu(q_raw.astype(f32))
    v = i_raw.astype(f32)

    def chunks(a, dim):
        return a.reshape(B, N, C, H, dim).transpose(1, 0, 3, 2, 4)

    xs = (chunks(q, K), chunks(k, K), chunks(log_f, K), chunks(v, V))
    causal = jnp.tril(jnp.ones((C, C), dtype=bool))[:, :, None]

    def step(S, inp):
        qc, kc, gc, vc = inp
        G = jnp.cumsum(gc, axis=-2)
        G_last = G[..., -1:, :]
        rel = G[..., :, None, :] - G[..., None, :, :]
        decay = jnp.exp(jnp.where(causal, rel, NEG_INF))
        A = jnp.einsum('bhtk,bhsk,bhtsk->bhts', qc, kc, decay)
        o = (jnp.einsum('bhts,bhsv->bhtv', A, vc)
             + jnp.einsum('bhtk,bhkv->bhtv', qc * jnp.exp(G), S))
        S = (jnp.exp(G_last)[..., 0, :, None] * S
             + jnp.einsum('bhsk,bhsv->bhkv', kc * jnp.exp(G_last - G), vc))
        return S, o

    S0 = jnp.zeros((B, H, K, V), f32)
    _, o = lax.scan(step, S0, xs)
    o = o.transpose(1, 0, 3, 2, 4).reshape(B, T, H, V)
    o = o * lax.rsqrt(jnp.mean(o * o, axis=-1, keepdims=True) + NORM_EPS) * norm_g.astype(f32)
    o = o * jax.nn.silu(g_raw.astype(f32).reshape(B, T, H, V))
    return o.reshape(B, T, H * V).astype(q_raw.dtype)


def banded_causal_attention(q, k, v, steps):
    P = steps
    L, Dh = q.shape[-2], q.shape[-1]
    lead = q.shape[:-2]
    n = -(-L // P)
    Lp = n * P
    nd = len(lead)
    qp = jnp.pad(q, [(0, 0)] * nd + [(0, Lp - L), (0, 0)])
    kp = jnp.pad(k, [(0, 0)] * nd + [(P, Lp - L), (0, 0)])
    vp = jnp.pad(v, [(0, 0)] * nd + [(P, Lp - L), (0, 0)])
    qb = qp.reshape(lead + (n, P, Dh))

    def band(a):
        return jnp.concatenate([a[..., :Lp, :].reshape(lead + (n, P, Dh)),
                                a[..., P:, :].reshape(lead + (n, P, Dh))], axis=-2)

    kb, vb = band(kp), band(vp)
    s = jnp.einsum('...nqd,...nkd->...nqk', qb, kb).astype(jnp.float32) * (Dh ** -0.5)
    q_pos = jnp.arange(n)[:, None] * P + jnp.arange(P)[None, :]
    k_pos = jnp.arange(n)[:, None] * P - P + jnp.arange(2 * P)[None, :]
    rel = q_pos[:, :, None] - k_pos[:, None, :]
    mask = (rel >= 0) & (rel <= steps) & (k_pos[:, None, :] >= 0)
    s = jnp.where(mask, s, NEG_INF)
    lse = jax.nn.logsumexp(s, axis=-1)
    p = jnp.exp(s - lse[..., None])
    o = jnp.einsum('...nqk,...nkd->...nqd', p, vb.astype(jnp.float32))
    o = o.reshape(lead + (Lp, Dh))[..., :L, :]
    lse = lse.reshape(lead + (Lp,))[..., :L]
    return o, lse


def dilated_attention_branch(q_raw, k_raw, v_raw):
    B, T, _ = q_raw.shape
    G, Hg, Dh = len(ATTN_GROUPS), ATTN_HEADS_PER_GROUP, HEAD_DIM
    q = rope(q_raw.reshape(B, T, G * Hg, Dh)).reshape(B, T, G, Hg, Dh)
    k = rope(k_raw.reshape(B, T, G * Hg, Dh)).reshape(B, T, G, Hg, Dh)
    v = v_raw.reshape(B, T, G, Hg, Dh)
    outs, lses = [], []
    for g, (window, dil) in enumerate(ATTN_GROUPS):
        L = T // dil

        def strided(a):
            return a.reshape(B, L, dil, Hg, Dh).transpose(0, 2, 3, 1, 4)

        o, lse = banded_causal_attention(strided(q[:, :, g]), strided(k[:, :, g]),
                                         strided(v[:, :, g]), window // dil)
        outs.append(o.transpose(0, 3, 1, 2, 4).reshape(B, T, Hg, Dh))
        lses.append(lse.transpose(0, 3, 1, 2).reshape(B, T, Hg))
    alpha = jax.nn.softmax(jnp.stack(lses, axis=0), axis=0)
    o = jnp.sum(alpha[..., None] * jnp.stack(outs, axis=0), axis=0)
    return o.reshape(B, T, Hg * Dh).astype(q_raw.dtype)


def conv_ffn(u, w_up, conv_w, conv_b, w_down):
    hup = u @ w_up
    C = hup.shape[-1]
    hc = lax.conv_general_dilated(hup, conv_w[:, None, :], window_strides=(1,),
                                  padding=[(CONV_WIDTH - 1, 0)],
                                  dimension_numbers=('NWC', 'WIO', 'NWC'),
                                  feature_group_count=C) + conv_b
    a, b = hc[..., :D_FF], hc[..., D_FF:]
    return (jax.nn.gelu(a) * b) @ w_down


def setup_inputs(seed: int = 0) -> dict:
    key = jax.random.key(seed)
    ks = jax.random.split(key, 14)
    nrm = jax.random.normal
    f32 = jnp.float32
    return {
        'x': nrm(ks[0], (BATCH, SEQ, D_MODEL), f32),
        'norm1_g': 1.0 + 0.02 * nrm(ks[1], (DEPTH, D_MODEL), f32),
        'w_in': nrm(ks[2], (DEPTH, D_MODEL, IN_COLS), f32) * D_MODEL ** -0.5,
        'hg_lb_param': 0.5 * nrm(ks[3], (DEPTH + 1, HG_WIDTH), f32),
        'hg_norm_g': 1.0 + 0.02 * nrm(ks[4], (DEPTH, HG_VDIM), f32),
        'w_proj_a': nrm(ks[5], (DEPTH, HG_WIDTH, D_MODEL), f32) * HG_WIDTH ** -0.5,
        'w_proj_b': nrm(ks[6], (DEPTH, ATTN_OUT, D_MODEL), f32) * ATTN_OUT ** -0.5,
        'w_out': nrm(ks[7], (DEPTH, D_MODEL, D_MODEL), f32) * D_MODEL ** -0.5,
        'norm2_g': 1.0 + 0.02 * nrm(ks[8], (DEPTH, D_MODEL), f32),
        'w_up': nrm(ks[9], (DEPTH, D_MODEL, 2 * D_FF), f32) * D_MODEL ** -0.5,
        'conv_w': nrm(ks[10], (DEPTH, CONV_WIDTH, 2 * D_FF), f32) * CONV_WIDTH ** -0.5,
        'conv_b': 0.01 * nrm(ks[11], (DEPTH, 2 * D_FF), f32),
        'w_down': nrm(ks[12], (DEPTH, D_FF, D_MODEL), f32) * D_FF ** -0.5,
        'norm_f_g': 1.0 + 0.02 * nrm(ks[13], (D_MODEL,), f32),
    }


def reference(x, norm1_g, w_in, hg_lb_param, hg_norm_g, w_proj_a, w_proj_b, w_out,
              norm2_g, w_up, conv_w, conv_b, w_down, norm_f_g):
    lower_bounds = jnp.cumsum(jax.nn.softmax(hg_lb_param.astype(jnp.float32), axis=0), axis=0)
    sizes = (HG_WIDTH,) * 4 + (ATTN_WIDTH,) * 3 + (D_MODEL,) * 2
    offsets = [int(o) for o in np.cumsum(sizes)[:-1]]
    h = x
    for l in range(DEPTH):
        u = rmsnorm(h, norm1_g[l])
        proj = u @ w_in[l]
        hq, hf, hi, hg, aq, ak, av, gate_a, gate_b = jnp.split(proj, offsets, axis=-1)
        ya = hgrn2_branch(hq, hf, hi, hg, lower_bounds[l], hg_norm_g[l]) @ w_proj_a[l]
        yb = dilated_attention_branch(aq, ak, av) @ w_proj_b[l]
        mix = jax.nn.sigmoid(gate_a) * ya + jax.nn.sigmoid(gate_b) * yb
        h = h + (mix @ w_out[l]).astype(h.dtype)
        v = rmsnorm(h, norm2_g[l])
        h = h + conv_ffn(v, w_up[l], conv_w[l], conv_b[l], w_down[l]).astype(h.dtype)
    return rmsnorm(h, norm_f_g)
```

```python
import math
from contextlib import ExitStack
import numpy as np
import concourse.bass as bass
import concourse.mybir as mybir
from concourse.bass_utils import run_bass_kernel_spmd

F32 = mybir.dt.float32
BF16 = mybir.dt.bfloat16
AF = mybir.ActivationFunctionType
ALU = mybir.AluOpType

D = 2048
T = 2048
E0 = 896
NE = 1152
EPS = 1e-6
DFF = 5632
NFB = 44
DIL = (1, 4, 16)
NSLAB = 300
NDS = 8
ENGS = ['pe', 'act', 'dve', 'pool', 'sp']
ENGNAME = {'pe': 'tensor', 'act': 'scalar', 'dve': 'vector', 'pool': 'gpsimd', 'sp': 'sync'}


def slab_plan():
    idx = {}
    n = 0
    for nm in ('hq', 'hf', 'hi', 'hg'):
        for h in range(16):
            idx[(nm, h)] = n; n += 1
    for nm in ('aq', 'ak'):
        for g in range(3):
            for p in range(2):
                for ab in range(2):
                    idx[(nm, g, p, ab)] = n; n += 1
    for g in range(3):
        for p in range(2):
            for half in range(2):
                idx[('av', g, p, half)] = n; n += 1
    for nm in ('ga', 'gb'):
        for cc in range(16):
            idx[(nm, cc)] = n; n += 1
    for cc in range(16):
        idx[('pa', cc)] = n; n += 1
    for q in range(4):
        idx[('pb', q)] = n; n += 1
    for cb in range(4):
        for kq in range(4):
            idx[('out', cb, kq)] = n; n += 1
    for fb in range(NFB):
        for ab in range(2):
            idx[('up', fb, ab)] = n; n += 1
    for fb in range(NFB):
        idx[('dn', fb)] = n; n += 1
    assert n == NSLAB
    return idx


class Op:
    __slots__ = ('eng', 'fn', 'is_dma', 'needs_sig', 'deps', 'dslot', 'dval', 'sigval')


class Sched:
    def __init__(self):
        self.ops = {e: [] for e in ENGS}
        self.lastw = {}
        self.readers = {}
        self.ndma = {e: 0 for e in ENGS}
        self.dma_hist = {e: [] for e in ENGS}
        self.bar_deps = []
        self.bar_pending = set()
        self.last_c = {}
        self.dma_since = []

    def _add(self, eng, fn, r, w, is_dma):
        op = Op()
        op.eng = eng; op.fn = fn; op.is_dma = is_dma; op.needs_sig = False
        op.deps = []; op.dslot = 0; op.dval = 0; op.sigval = 0
        deps = []
        if eng in self.bar_pending:
            deps = list(self.bar_deps)
            self.bar_pending.discard(eng)
        for k in r:
            lw = self.lastw.get(k)
            if lw is not None:
                deps.append(lw)
        for k in w:
            lw = self.lastw.get(k)
            if lw is not None:
                deps.append(lw)
            rd = self.readers.get(k)
            if rd:
                deps.extend(rd.values())
        if is_dma:
            i = self.ndma[eng]
            op.dslot = i % NDS
            op.dval = 16 * (i // NDS + 1)
            self.ndma[eng] += 1
            if i >= NDS:
                deps.append(self.dma_hist[eng][i - NDS])
            self.dma_hist[eng].append(op)
            self.dma_since.append(op)
        else:
            self.last_c[eng] = op
        seen = set()
        for d in deps:
            if d is op or id(d) in seen:
                continue
            seen.add(id(d))
            if (not d.is_dma) and d.eng == eng and eng == 'pe' and not is_dma:
                continue
            d.needs_sig = True
            op.deps.append(d)
        for k in r:
            rd = self.readers.setdefault(k, {})
            rd[(eng, id(op)) if is_dma else eng] = op
        for k in w:
            self.lastw[k] = op
            self.readers[k] = {}
        self.ops[eng].append(op)
        return op

    def op(self, eng, fn, r=(), w=()):
        return self._add(eng, fn, r, w, False)

    def dma(self, eng, fn, r=(), w=()):
        return self._add(eng, fn, r, w, True)

    def barrier(self):
        deps = list(self.last_c.values()) + list(self.dma_since)
        for d in deps:
            d.needs_sig = True
        self.bar_deps = deps
        self.bar_pending = set(ENGS)
        self.dma_since = []
        self.lastw = {}
        self.readers = {}

    def emit(self, nc, st):
        for e in ENGS:
            c = 0
            for op in self.ops[e]:
                if (not op.is_dma) and op.needs_sig:
                    c += 1
                    op.sigval = c
        sems = {e: st.enter_context(nc.semaphore("cs_" + e)) for e in ENGS}
        dsems = {e: [st.enter_context(nc.semaphore("ds_%s%d" % (e, i))) for i in range(NDS)]
                 for e in ENGS if self.ndma[e] > 0}
        block = st.enter_context(nc.Block())
        for e in ENGS:
            def body(eng, e=e):
                waited = {}
                for op in self.ops[e]:
                    for d in op.deps:
                        if d.is_dma:
                            key = (d.eng, d.dslot); val = d.dval; sem = dsems[d.eng][d.dslot]
                        else:
                            key = d.eng; val = d.sigval; sem = sems[d.eng]
                        if waited.get(key, 0) >= val:
                            continue
                        waited[key] = val
                        eng.wait_ge(sem, val)
                    ins = op.fn(eng)
                    if op.is_dma:
                        ins.then_inc(dsems[e][op.dslot], 16)
                    elif op.needs_sig:
                        ins.then_inc(sems[e], 1)
            getattr(block, ENGNAME[e])(body)


class Ring:
    def __init__(self, name, bufs):
        self.name = name; self.bufs = bufs; self.i = 0

    def next(self):
        s = self.i % len(self.bufs)
        self.i += 1
        return s


class _Stop(Exception):
    pass


def build_nc(debug=False, stop_after=0):
    nc = bass.Bass("TRN2", target_bir_lowering=False)
    SI = slab_plan()
    S = Sched()
    mask_specs = []

    def mask_idx(lo, hi, vm):
        key = (lo, hi, vm)
        if key not in mask_specs:
            mask_specs.append(key)
        return mask_specs.index(key)

    NMASK = 12
    xc = nc.dram_tensor("xc", [T, D], F32, kind="ExternalInput").ap()
    wall = nc.dram_tensor("wall", [NSLAB, 128, 2048], F32, kind="ExternalInput").ap()
    g1T_d = nc.dram_tensor("g1T", [128, 16], F32, kind="ExternalInput").ap()
    g2T_d = nc.dram_tensor("g2T", [128, 16], F32, kind="ExternalInput").ap()
    gfb_d = nc.dram_tensor("gfb", [128, D], F32, kind="ExternalInput").ap()
    lbp_d = nc.dram_tensor("lbp", [128, 32], F32, kind="ExternalInput").ap()
    hng_d = nc.dram_tensor("hng", [128, 1], F32, kind="ExternalInput").ap()
    convp_d = nc.dram_tensor("convp", [128, 88 * 4], F32, kind="ExternalInput").ap()
    cs_d = nc.dram_tensor("cs", [128, 2 * T], F32, kind="ExternalInput").ap()
    masks_d = nc.dram_tensor("masks", [128, NMASK * 128], F32, kind="ExternalInput").ap()
    ident_d = nc.dram_tensor("ident", [128, 128], F32, kind="ExternalInput").ap()
    valid_d = nc.dram_tensor("valid", [128, 1], F32, kind="ExternalInput").ap()
    out_d = nc.dram_tensor("y", [1024, D], F32, kind="ExternalOutput").ap()
    if debug:
        dbgA = nc.dram_tensor("dbgA", [128, 16 * NE], F32, kind="ExternalOutput").ap()
        dbgB = nc.dram_tensor("dbgB", [128, 4 * NE], F32, kind="ExternalOutput").ap()
        dbgH = nc.dram_tensor("dbgH", [1024, D], F32, kind="ExternalOutput").ap()

    st = ExitStack()
    try:
      with st:
        def maybe_stop(k):
            if stop_after == k:
                S.barrier()
                S.op('sp', lambda e: e.nop())
                S.emit(nc, st)
                raise _Stop()
        def sb(name, shape, dt):
            return st.enter_context(nc.sbuf_tensor(name, shape, dt))

        RA = sb("RA", [128, 16384], F32)
        RW = sb("RW", [128, 20544], F32)
        BoT = sb("BoT", [128, 4 * NE], BF16)
        RINGA = sb("ringA", [128, 3 * 2048], BF16)
        STG = [sb("stg%d" % i, [128, 1024], F32) for i in range(2)]
        RINGB = sb("ringB", [128, 6 * 2048], BF16)
        ident_f = sb("ident_fs", [128, 128], F32)
        ident = sb("ident_bs", [128, 128], BF16)
        onesm = sb("onesm", [128, 128], BF16)
        ones1 = sb("ones1", [128, 128], BF16)
        g1T = sb("g1Ts", [128, 16], F32)
        g2T = sb("g2Ts", [128, 16], F32)
        lbp = sb("lbps", [128, 32], F32)
        lb = sb("lb_s", [128, 16], F32)
        oml = sb("oml_s", [128, 16], F32)
        hng = sb("hngs", [128, 1], F32)
        valid = sb("valids", [128, 1], F32)
        convp = sb("convps", [128, 88 * 4], F32)
        ss = sb("ss", [128, 32], F32)
        rs = sb("rs", [128, 32], F32)
        small = sb("small", [128, 5 * 32], F32)
        Sst = sb("Sst", [128, 2 * 128], F32)
        epsT = sb("epsT", [128, 1], F32)
        PS = [st.enter_context(nc.psum_tensor("ps%d" % i, [128, 512], F32)) for i in range(8)]

        uT = RA[:, :].bitcast(BF16).rearrange("p (k t) -> p k t", k=16)
        h1 = RA[:, :].rearrange("p (j d) -> p j d", j=8)
        ringA = Ring('rA', [RINGA[:, i * 2048:(i + 1) * 2048] for i in range(3)])
        ringB = Ring('rB', [RINGB[:, i * 2048:(i + 1) * 2048] for i in range(6)])

        class Carver:
            def __init__(self, base=0):
                self.off = base

            def get(self, n_elems, dt):
                nby = n_elems * (4 if dt == F32 else 2)
                nw = (nby + 3) // 4
                nw = (nw + 7) // 8 * 8
                a = RW[:, self.off:self.off + nw]
                self.off += nw
                assert self.off <= 20544, self.off
                if dt == BF16:
                    return a.bitcast(BF16)[:, :n_elems]
                return a[:, :n_elems]

        dctr = [0]
        mctr = [0]

        def Dbank():
            b = dctr[0] % 4
            dctr[0] += 1
            return b

        def Mbank():
            b = 4 + mctr[0] % 4
            mctr[0] += 1
            return b

        def psk(b):
            return ('ps', b)

        def stsl(start, n, step=1):
            return slice(start, start + (n - 1) * step + 1, step)

        stg_ctr = [0]

        def load_slab(ring, key, ncols=2048):
            slot = ring.next()
            buf = ring.bufs[slot]
            si = SI[key]
            for hf in range(2):
                k = stg_ctr[0] % 2
                stg_ctr[0] += 1
                c0 = hf * 1024
                S.dma('sp', lambda e, k=k, c0=c0: e.dma_start(out=STG[k][:, :], in_=wall[si, :, c0:c0 + 1024]),
                      r=(), w=[('stg', k)])
                S.op('pool', lambda e, k=k, c0=c0: e.tensor_copy(out=buf[:, c0:c0 + 1024], in_=STG[k][:, :]),
                     r=[('stg', k)], w=[(ring.name, slot)])
            return buf, (ring.name, slot)

        def dense_fm(slabs, KC, rhs_fn, blocks, evac, rres=()):
            for bi, (t0, n) in enumerate(blocks):
                bks = []
                for (buf, rk) in slabs:
                    bk = Dbank()
                    bks.append(bk)
                    for kc in range(KC):
                        S.op('pe', lambda e, bk=bk, buf=buf, kc=kc, t0=t0, n=n: e.matmul(
                            out=PS[bk][:, :n], lhsT=buf[:, kc * 128:(kc + 1) * 128],
                            rhs=rhs_fn(kc, t0, n), start=(kc == 0), stop=(kc == KC - 1)),
                            r=[rk] + list(rres), w=[psk(bk)])
                evac(bi, (t0, n), [PS[b] for b in bks], [psk(b) for b in bks])

        FBLK = [(512 * i, 512) for i in range(4)]
        EBLK = [(E0 + 384 * i, 384) for i in range(3)]

        def uT_rhs(kc, t0, n):
            return uT[:, kc, t0:t0 + n]

        S.dma('sp', lambda e: e.dma_start(out=ident_f[:, :], in_=ident_d[:, :]), w=['ident_f'])
        S.dma('sp', lambda e: e.dma_start(out=g1T[:, :], in_=g1T_d[:, :]), w=['g1T'])
        S.dma('sp', lambda e: e.dma_start(out=g2T[:, :], in_=g2T_d[:, :]), w=['g2T'])
        S.dma('sp', lambda e: e.dma_start(out=lbp[:, :], in_=lbp_d[:, :]), w=['lbp'])
        S.dma('sp', lambda e: e.dma_start(out=hng[:, :], in_=hng_d[:, :]), w=['hng'])
        S.dma('sp', lambda e: e.dma_start(out=valid[:, :], in_=valid_d[:, :]), w=['valid'])
        S.dma('sp', lambda e: e.dma_start(out=convp[:, :], in_=convp_d[:, :]), w=['convp'])
        S.op('dve', lambda e: e.tensor_copy(out=ident[:, :], in_=ident_f[:, :]), r=['ident_f'], w=['ident'])
        S.op('pool', lambda e: e.memset(onesm[:, :], 1.0 / 128.0), w=['onesm'])
        S.op('pool', lambda e: e.memset(ones1[:, :], 1.0), w=['ones1'])
        S.op('pool', lambda e: e.memset(epsT[:, :], EPS), w=['epsT'])
        S.op('dve', lambda e: e.tensor_tensor(out=lb[:, :], in0=lbp[:, 0:16], in1=lbp[:, 16:32], op=ALU.subtract),
             r=['lbp'], w=['lb'])
        S.op('act', lambda e: e.activation(out=lb[:, :], in_=lb[:, :], func=AF.Sigmoid), r=['lb'], w=['lb'])
        S.op('dve', lambda e: e.tensor_scalar(out=oml[:, :], in0=lb[:, :], scalar1=-1.0, scalar2=1.0,
                                              op0=ALU.mult, op1=ALU.add), r=['lb'], w=['oml'])

        def norm_transpose(src_fn, ntiles, gT, dst_fn, tagbase, load=None, base=0):
            cv = Carver(base)
            xt = [cv.get(2048, F32) for _ in range(2)] if load is not None else None
            xn = [cv.get(2048, BF16) for _ in range(2)]
            junk = cv.get(2048, BF16)
            for i in range(ntiles):
                par = i % 2
                if load is not None:
                    S.dma('sp', lambda e, i=i, par=par: e.dma_start(out=xt[par][:, :], in_=load(i)),
                          w=[('xt', par)])
                    src = xt[par]
                    sres = [('xt', par)]
                else:
                    src = src_fn(i)
                    sres = []
                col = tagbase + i
                S.op('act', lambda e, src=src, col=col: e.activation(
                    out=junk[:, :], in_=src, func=AF.Square, accum_out=ss[:, col:col + 1]),
                    r=sres, w=['junk', ('ss', col)])
                S.op('dve', lambda e, col=col: e.tensor_scalar(
                    out=rs[:, col:col + 1], in0=ss[:, col:col + 1], scalar1=1.0 / D, scalar2=EPS,
                    op0=ALU.mult, op1=ALU.add), r=[('ss', col)], w=[('rs', col)])
                S.op('act', lambda e, col=col: e.activation(
                    out=rs[:, col:col + 1], in_=rs[:, col:col + 1], func=AF.Sqrt),
                    r=[('rs', col)], w=[('rs', col)])
                S.op('dve', lambda e, col=col: e.reciprocal(out=rs[:, col:col + 1], in_=rs[:, col:col + 1]),
                     r=[('rs', col)], w=[('rs', col)])
                S.op('act', lambda e, src=src, col=col, par=par: e.activation(
                    out=xn[par][:, :], in_=src, func=AF.Copy, scale=rs[:, col:col + 1]),
                    r=sres + [('rs', col)], w=[('xn', par)])
                for half in range(2):
                    bk = Mbank()
                    psb = PS[bk][:, :].bitcast(BF16)
                    for j in range(8):
                        kc = half * 8 + j
                        S.op('pe', lambda e, psb=psb, j=j, kc=kc, par=par: e.transpose(
                            out=psb[:, j * 128:(j + 1) * 128], in_=xn[par][:, kc * 128:(kc + 1) * 128],
                            identity=ident[:, :]), r=[('xn', par), 'ident'], w=[psk(bk)])
                    dst_fn(i, half, psb.rearrange("p (k t) -> p k t", k=8), psk(bk), gT)

        def p1_dst(i, half, psv, pk, gT):
            S.op('dve', lambda e: e.tensor_tensor(
                out=uT[:, half * 8:half * 8 + 8, i * 128:(i + 1) * 128], in0=psv,
                in1=gT[:, half * 8:half * 8 + 8].unsqueeze(2).to_broadcast([128, 8, 128]), op=ALU.mult),
                r=[pk, 'g1T'], w=[])

        norm_transpose(None, 16, g1T, p1_dst, 0, load=lambda i: xc[i * 128:(i + 1) * 128, :])
        S.barrier()
        maybe_stop(1)

        cv = Carver(0)
        CS = cv.get(2 * T, F32)
        MK = cv.get(NMASK * 128, F32)
        R1k = cv.get(T, BF16); R2k = cv.get(T, BF16)
        R1q = [cv.get(NE, BF16) for _ in range(2)]; R2q = [cv.get(NE, BF16) for _ in range(2)]
        Vt = cv.get(16 * 256, BF16)
        OD = cv.get(4 * NE, F32).rearrange("p (s t) -> p s t", s=4)
        tmp = [cv.get(512, F32) for _ in range(4)]
        EX = [cv.get(512, BF16) for _ in range(2)]
        PT = [cv.get(512, BF16) for _ in range(2)]
        S.dma('sp', lambda e: e.dma_start(out=CS[:, :], in_=cs_d[:, :]), w=['CS'])
        for qb in R1q + R2q:
            S.op('pool', lambda e, qb=qb: e.memset(qb[:, :], 0.0), w=['Rq'])
        S.dma('sp', lambda e: e.dma_start(out=MK[:, :], in_=masks_d[:, :]), w=['MK'])
        cosT = CS[:, 0:T]
        sinT = CS[:, T:2 * T]
        SCALE = 1.0 / math.sqrt(128.0)
        blkctr = [0]

        def rope_evac(R1, R2, tbase, rtag):
            def ev(bi, blk, pss, pks):
                t0, n = blk
                A, B = pss[0][:, :n], pss[1][:, :n]
                c_, s_ = cosT[:, t0:t0 + n], sinT[:, t0:t0 + n]
                S.op('dve', lambda e: e.tensor_tensor(out=tmp[0][:, :n], in0=A, in1=c_, op=ALU.mult),
                     r=[pks[0], 'CS'], w=[('tmp', 0)])
                S.op('dve', lambda e: e.tensor_tensor(out=tmp[1][:, :n], in0=B, in1=s_, op=ALU.mult),
                     r=[pks[1], 'CS'], w=[('tmp', 1)])
                for (dst, p0, p1) in (R1 if isinstance(R1, list) else [(R1, 0, 128)]):
                    S.op('pool', lambda e, dst=dst, p0=p0, p1=p1: e.tensor_tensor(
                        out=dst[p0:p1, t0 - tbase:t0 - tbase + n], in0=tmp[0][p0:p1, :n],
                        in1=tmp[1][p0:p1, :n], op=ALU.subtract),
                        r=[('tmp', 0), ('tmp', 1)], w=[rtag])
                S.op('dve', lambda e: e.tensor_tensor(out=tmp[2][:, :n], in0=B, in1=c_, op=ALU.mult),
                     r=[pks[1], 'CS'], w=[('tmp', 2)])
                S.op('dve', lambda e: e.tensor_tensor(out=tmp[3][:, :n], in0=A, in1=s_, op=ALU.mult),
                     r=[pks[0], 'CS'], w=[('tmp', 3)])
                for (dst, p0, p1) in (R2 if isinstance(R2, list) else [(R2, 0, 128)]):
                    S.op('pool', lambda e, dst=dst, p0=p0, p1=p1: e.tensor_tensor(
                        out=dst[p0:p1, t0 - tbase:t0 - tbase + n], in0=tmp[2][p0:p1, :n],
                        in1=tmp[3][p0:p1, :n], op=ALU.add),
                        r=[('tmp', 2), ('tmp', 3)], w=[rtag])
            return ev

        for p in range(2):
            for g in range(3):
                d = DIL[g]
                tk0 = 768 if g == 0 else 0
                kblocks = [(768, 512), (1280, 512), (1792, 256)] if g == 0 else FBLK
                sl = [load_slab(ringA, ('ak', g, p, ab)) for ab in range(2)]
                dense_fm(sl, 16, uT_rhs, kblocks, rope_evac(R1k, R2k, tk0, 'Rk'))
                sl = [load_slab(ringA, ('aq', g, p, ab)) for ab in range(2)]
                dense_fm(sl, 16, uT_rhs, EBLK, rope_evac([(R1q[0], 0, 64), (R1q[1], 64, 128)],
                                                         [(R2q[0], 0, 64), (R2q[1], 64, 128)], E0, 'Rq'))
                vs = [load_slab(ringB, ('av', g, p, half)) for half in range(2)]
                if g == 0:
                    vblocks = [(tt - 6, (128 * tt, 1)) for tt in range(6, 16)]
                elif g == 1:
                    vblocks = [(r * 4 + j, (r + 512 * j, 4)) for r in range(4) for j in range(4)]
                else:
                    vblocks = [(r, (r, 16)) for r in range(16)]
                for (vi, (tstart, step)) in vblocks:
                    bk = Dbank()
                    for kc in range(16):
                        buf, rk = vs[kc // 8]
                        kl = kc % 8
                        S.op('pe', lambda e, bk=bk, kc=kc, buf=buf, kl=kl, tstart=tstart, step=step: e.matmul(
                            out=PS[bk][:, :256], lhsT=uT[:, kc, stsl(tstart, 128, step)],
                            rhs=buf[:, kl * 256:(kl + 1) * 256], start=(kc == 0), stop=(kc == 15)),
                            r=[rk], w=[psk(bk)])
                    S.op('act', lambda e, bk=bk, vi=vi: e.activation(
                        out=Vt[:, vi * 256:(vi + 1) * 256], in_=PS[bk][:, :256], func=AF.Copy),
                        r=[psk(bk)], w=['V'])
                if g == 0:
                    qblocks = []
                    for i in range(9):
                        m0 = E0 + 128 * i
                        kbs = []
                        ttp = m0 // 128 - 1
                        kbs.append((128 * ttp - 768, 1, ttp - 6, mask_idx(0, 999, 1 if ttp < 8 else 0)))
                        ttd = m0 // 128
                        kbs.append((128 * ttd - 768, 1, ttd - 6, mask_idx(-999, 0, 1 if ttd < 8 else 0)))
                        qblocks.append((128 * i, 1, 128, 128 * i, kbs))
                elif g == 1:
                    qblocks = []
                    for r in range(4):
                        for (m0, nq) in ((224, 32), (256, 128), (384, 128)):
                            kbs = []
                            j1 = m0 // 128
                            for j in (j1 - 1, j1):
                                off = m0 - 128 * j
                                if j == j1:
                                    mi = mask_idx(-999, off, 1 if j < 2 else 0)
                                else:
                                    mi = mask_idx(off - 128, 999, 1 if j < 2 else 0)
                                kbs.append((512 * j + r, 4, r * 4 + j, mi))
                            qs = 4 * m0 + r - E0
                            qblocks.append((qs, 4, nq, qs, kbs))
                else:
                    qblocks = []
                    for r in range(16):
                        kbs = [(r, 16, r, mask_idx(-999, 56, 2))]
                        qblocks.append((r, 16, 72, r, kbs))
                for (qs, qstep, nq, e0, kbs) in qblocks:
                    nkb = len(kbs)
                    bi = blkctr[0] % 2
                    blkctr[0] += 1
                    bk = Mbank()
                    for hd in range(2):
                        for ki, (ks, kstep, vi, mi) in enumerate(kbs):
                            slot = (hd * nkb + ki) * 128
                            for ri, (Rk_, Rq_) in enumerate(((R1k, R1q), (R2k, R2q))):
                                S.op('pe', lambda e, bk=bk, slot=slot, Rk_=Rk_, Rq_=Rq_, hd=hd, ks=ks, kstep=kstep,
                                     qs=qs, nq=nq, qstep=qstep, ri=ri: e.matmul(
                                    out=PS[bk][:, slot:slot + nq],
                                    lhsT=Rk_[:, stsl(ks, 128, kstep)],
                                    rhs=Rq_[hd][:, stsl(qs, nq, qstep)],
                                    start=(ri == 0), stop=(ri == 1)),
                                    r=['Rk', 'Rq'], w=[psk(bk)])
                    nsl = 2 * nkb
                    S.op('act', lambda e, bk=bk, bi=bi, nsl=nsl, nq=nq: e.activation(
                        out=EX[bi][:, :nsl * 128].rearrange("p (s q) -> p s q", s=nsl)[:, :, :nq],
                        in_=PS[bk][:, :nsl * 128].rearrange("p (s q) -> p s q", s=nsl)[:, :, :nq],
                        func=AF.Exp, scale=SCALE), r=[psk(bk)], w=[('EX', bi)])
                    for ki, (ks, kstep, vi, mi) in enumerate(kbs):
                        S.op('pool', lambda e, bi=bi, ki=ki, mi=mi, nkb=nkb, nq=nq: e.tensor_tensor(
                            out=PT[bi][:, :].rearrange("p (h k q) -> p h k q", h=2, k=2)[:, :, ki, :nq],
                            in0=EX[bi][:, :2 * nkb * 128].rearrange("p (h k q) -> p h k q", h=2, k=nkb)[:, :, ki, :nq],
                            in1=MK[:, mi * 128:mi * 128 + nq].unsqueeze(1).to_broadcast([128, 2, nq]),
                            op=ALU.mult), r=[('EX', bi), 'MK'], w=[('PT', bi)])
                    bk2 = Mbank()
                    for hd in range(2):
                        for which in range(2):
                            for ki, (ks, kstep, vi, mi) in enumerate(kbs):
                                oslot = (which * 2 + hd) * 128
                                S.op('pe', lambda e, bk2=bk2, oslot=oslot, which=which, hd=hd, vi=vi, ki=ki, bi=bi,
                                     nq=nq, nkb=nkb: e.matmul(
                                    out=PS[bk2][:, oslot:oslot + nq],
                                    lhsT=(Vt[:, vi * 256 + hd * 128:vi * 256 + (hd + 1) * 128] if which == 0
                                          else ones1[:, :]),
                                    rhs=PT[bi][:, :].rearrange("p (h k q) -> p h k q", h=2, k=2)[:, hd, ki, :nq],
                                    start=(ki == 0), stop=(ki == nkb - 1)),
                                    r=[('PT', bi), 'V', 'ones1'], w=[psk(bk2)])
                    psv = PS[bk2][:, :].rearrange("p (s q) -> p s q", s=4)[:, :, :nq]
                    odv = OD[:, :, stsl(e0, nq, d)]
                    if g == 0:
                        S.op('act', lambda e, psv=psv, odv=odv: e.activation(out=odv, in_=psv, func=AF.Copy),
                             r=[psk(bk2)], w=['OD'])
                    else:
                        S.op('dve', lambda e, psv=psv, odv=odv: e.tensor_tensor(out=odv, in0=psv, in1=odv,
                                                                                op=ALU.add),
                             r=[psk(bk2), 'OD'], w=['OD'])
            S.op('dve', lambda e: e.tensor_scalar_add(out=OD[:, 2:4, :], in0=OD[:, 2:4, :], scalar1=1e-30),
                 r=['OD'], w=['OD'])
            S.op('dve', lambda e: e.reciprocal(out=OD[:, 2:4, :], in_=OD[:, 2:4, :]), r=['OD'], w=['OD'])
            S.op('dve', lambda e, p=p: e.tensor_tensor(
                out=BoT[:, 2 * p * NE:(2 * p + 2) * NE].rearrange("p (h t) -> p h t", h=2),
                in0=OD[:, 0:2, :], in1=OD[:, 2:4, :], op=ALU.mult), r=['OD'], w=['OD'])
        S.barrier()

        maybe_stop(2)
        cv = Carver(0)
        AoT = cv.get(16 * NE, BF16)
        B1 = cv.get(T, F32); B2 = cv.get(T, F32); B3 = cv.get(T, F32)
        rbo = [0]

        def rb_get(n, dt):
            n2 = n if dt == BF16 else 2 * n
            a = RINGB[:, rbo[0]:rbo[0] + n2]
            rbo[0] += (n2 + 15) // 16 * 16
            assert rbo[0] <= 6 * 2048
            return a if dt == BF16 else a.bitcast(F32)

        KgT = rb_get(NE, BF16); KdT = rb_get(T, BF16)
        vT_ = rb_get(T, BF16); vtok = rb_get(T, BF16)
        Kd_e = rb_get(T, BF16); Kd_o = rb_get(T, BF16)
        QgT = cv.get(NE, BF16); gsT = cv.get(NE, BF16)
        Stil = cv.get(18 * 128, BF16)
        AT = cv.get(9 * 128, BF16)
        osq = cv.get(384, BF16)
        T1 = B1[:, 0:NE]
        osb = B3[:, 0:384]; rstd = B3[:, 384:768]
        scanm = cv.get(T, BF16)
        tri = cv.get(128, F32)
        GL = small[:, 0:32]; Gm = small[:, 32:64]; dec = small[:, 64:96]; egm = small[:, 96:128]
        EL = small[:, 128:160]
        S.op('pool', lambda e: e.memset(scanm[:, :], 1.0), w=['scanm'])
        S.op('pool', lambda e: e.memset(scanm[:, 0:T:64], 0.0), w=['scanm'])
        S.op('pool', lambda e: e.memset(AT[:, :], 0.0), w=['AT'])
        S.op('pool', lambda e: e.memset(Kd_e[:, :], 0.0), w=['Kd'])
        S.op('pool', lambda e: e.memset(Kd_o[:, :], 0.0), w=['Kd'])
        S.dma('sp', lambda e: e.dma_start(out=tri[:, :], in_=masks_d[:, (NMASK - 1) * 128:NMASK * 128]), w=['tri'])
        B3v = B3.rearrange("p (c j) -> p c j", j=64)

        for h in range(16):
            sf = load_slab(ringA, ('hf', h))
            si_ = load_slab(ringA, ('hi', h))

            def ev_f(bi, blk, pss, pks):
                t0, n = blk
                S.op('act', lambda e: e.activation(out=B1[:, t0:t0 + n], in_=pss[0][:, :n], func=AF.Sigmoid),
                     r=[pks[0]], w=['B1'])
            dense_fm([sf], 16, uT_rhs, FBLK, ev_f)

            def ev_i(bi, blk, pss, pks):
                t0, n = blk
                S.op('act', lambda e: e.activation(out=vT_[:, t0:t0 + n], in_=pss[0][:, :n], func=AF.Copy),
                     r=[pks[0]], w=['vT'])
            dense_fm([si_], 16, uT_rhs, FBLK, ev_i)
            S.op('dve', lambda e, h=h: e.tensor_scalar(out=B1[:, :], in0=B1[:, :], scalar1=oml[:, h:h + 1],
                                                       scalar2=lb[:, h:h + 1], op0=ALU.mult, op1=ALU.add),
                 r=['B1', 'oml', 'lb'], w=['B1'])
            S.op('act', lambda e: e.activation(out=B2[:, :], in_=B1[:, :], func=AF.Ln), r=['B1'], w=['B2'])
            S.op('dve', lambda e: e.tensor_tensor_scan(out=B3[:, :], data0=scanm[:, :], data1=B2[:, :], initial=0.0,
                                                       op0=ALU.mult, op1=ALU.add), r=['B2', 'scanm'], w=['B3'])
            S.op('dve', lambda e: e.tensor_copy(out=GL, in_=B3v[:, :, 63]), r=['B3'], w=['GL'])
            S.op('dve', lambda e: e.tensor_copy(out=Gm, in_=B3v[:, :, 31]), r=['B3'], w=['Gm'])
            S.op('dve', lambda e: e.tensor_tensor(out=B3v, in0=B3v, in1=Gm.unsqueeze(2).to_broadcast([128, 32, 64]),
                                                  op=ALU.subtract), r=['B3', 'Gm'], w=['B3'])
            S.op('act', lambda e: e.activation(out=dec, in_=GL, func=AF.Exp), r=['GL'], w=['dec'])
            S.op('act', lambda e: e.activation(out=egm, in_=Gm, func=AF.Exp), r=['Gm'], w=['egm'])
            S.op('act', lambda e: e.activation(out=EL, in_=B3v[:, :, 63], func=AF.Exp), r=['B3'], w=['EL'])
            S.op('act', lambda e: e.activation(out=B2[:, :], in_=B3[:, :], func=AF.Exp, scale=-1.0),
                 r=['B3'], w=['B2'])
            S.op('pool', lambda e: e.tensor_scalar(out=B1[:, :], in0=B1[:, :], scalar1=-1.0, scalar2=1.0,
                                                   op0=ALU.mult, op1=ALU.add), r=['B1', 'B2'], w=['B1'])
            S.op('pool', lambda e: e.tensor_tensor(out=B1[:, :], in0=B1[:, :], in1=B2[:, :], op=ALU.mult),
                 r=['B1', 'B2'], w=['B1'])
            S.op('pool', lambda e: e.tensor_copy(out=KgT[:, :], in_=B1[:, E0:T]), r=['B1'], w=['KgT'])
            S.op('pool', lambda e: e.tensor_tensor(
                out=KdT.rearrange("p (c j) -> p c j", j=64), in0=B1.rearrange("p (c j) -> p c j", j=64),
                in1=EL.unsqueeze(2).to_broadcast([128, 32, 64]), op=ALU.mult), r=['B1', 'EL'], w=['KdT'])
            S.op('act', lambda e: e.activation(out=B2[:, E0:T], in_=B3[:, E0:T], func=AF.Exp),
                 r=['B3', 'B1'], w=['B2'])
            sq = load_slab(ringA, ('hq', h))
            sg = load_slab(ringA, ('hg', h))

            def ev_q(bi, blk, pss, pks):
                t0, n = blk
                S.op('act', lambda e: e.activation(out=T1[:, t0 - E0:t0 - E0 + n], in_=pss[0][:, :n], func=AF.Silu),
                     r=[pks[0]], w=['B1'])
            dense_fm([sq], 16, uT_rhs, EBLK, ev_q)
            S.op('pool', lambda e: e.tensor_tensor(out=QgT[:, :], in0=T1[:, :], in1=B2[:, E0:T], op=ALU.mult),
                 r=['B1', 'B2'], w=['QgT'])

            def ev_g(bi, blk, pss, pks):
                t0, n = blk
                S.op('act', lambda e: e.activation(out=gsT[:, t0 - E0:t0 - E0 + n], in_=pss[0][:, :n], func=AF.Silu),
                     r=[pks[0]], w=['gsT'])
            dense_fm([sg], 16, uT_rhs, EBLK, ev_g)
            for (src, dst, sres, dres, eng) in ((vT_, vtok, 'vT', 'vtok', 'act'), (KdT, None, 'KdT', 'Kd', 'dve')):
                for half in range(2):
                    bk = Mbank()
                    psb = PS[bk][:, :].bitcast(BF16)
                    for j in range(8):
                        tl = half * 8 + j
                        S.op('pe', lambda e, psb=psb, j=j, tl=tl, src=src: e.transpose(
                            out=psb[:, j * 128:(j + 1) * 128], in_=src[:, tl * 128:(tl + 1) * 128],
                            identity=ident[:, :]), r=[sres, 'ident'], w=[psk(bk)])
                    if eng == 'act':
                        S.op('act', lambda e, psb=psb, half=half, dst=dst: e.activation(
                            out=dst[:, half * 1024:(half + 1) * 1024], in_=psb[:, :], func=AF.Copy),
                            r=[psk(bk)], w=[dres])
                    else:
                        S.op('dve', lambda e, psb=psb, half=half: e.tensor_copy(
                            out=Kd_e[0:64, half * 1024:(half + 1) * 1024], in_=psb[0:64, :]),
                            r=[psk(bk)], w=[dres])
                        S.op('dve', lambda e, psb=psb, half=half: e.tensor_copy(
                            out=Kd_o[64:128, half * 1024:(half + 1) * 1024], in_=psb[64:128, :]),
                            r=[psk(bk)], w=[dres])
            S.op('pool', lambda e: e.memset(Sst[:, 0:128], 0.0), r=[], w=[('S', 0)])
            for c in range(31):
                if c % 4 == 0:
                    ubk = Mbank()
                tl, par = c // 2, c % 2
                j = c % 4
                S.op('pe', lambda e, ubk=ubk, j=j, tl=tl, par=par: e.matmul(
                    out=PS[ubk][:, j * 128:(j + 1) * 128],
                    lhsT=(Kd_e if par == 0 else Kd_o)[:, tl * 128:(tl + 1) * 128],
                    rhs=vtok[:, tl * 128:(tl + 1) * 128], start=True, stop=True),
                    r=['Kd', 'vtok'], w=[psk(ubk)])
                so, sn = c % 2, (c + 1) % 2
                S.op('dve', lambda e, ubk=ubk, j=j, so=so, sn=sn, c=c: e.scalar_tensor_tensor(
                    out=Sst[:, sn * 128:(sn + 1) * 128], in0=Sst[:, so * 128:(so + 1) * 128],
                    scalar=dec[:, c:c + 1], in1=PS[ubk][:, j * 128:(j + 1) * 128], op0=ALU.mult, op1=ALU.add),
                    r=[psk(ubk), ('S', so), 'dec'], w=[('S', sn)])
                if c + 1 >= 14:
                    oc = c + 1
                    S.op('pool', lambda e, sn=sn, oc=oc: e.tensor_scalar(
                        out=Stil[:, (oc - 14) * 128:(oc - 13) * 128], in0=Sst[:, sn * 128:(sn + 1) * 128],
                        scalar1=egm[:, oc:oc + 1], scalar2=None, op0=ALU.mult),
                        r=[('S', sn), 'egm'], w=['Stil'])
            for ti in range(9):
                tl = 7 + ti
                if ti % 4 == 0:
                    abk = Mbank()
                col = (ti % 4) * 128
                q0 = ti * 128
                S.op('pe', lambda e, abk=abk, col=col, q0=q0: e.matmul(
                    out=PS[abk][:, col:col + 128], lhsT=KgT[:, q0:q0 + 128],
                    rhs=QgT[:, q0:q0 + 128], start=True, stop=True), r=['KgT', 'QgT'], w=[psk(abk)])
                for par in range(2):
                    S.op('dve', lambda e, abk=abk, col=col, ti=ti, par=par: e.tensor_tensor(
                        out=AT[par * 64:(par + 1) * 64, ti * 128 + par * 64:ti * 128 + par * 64 + 64],
                        in0=PS[abk][par * 64:(par + 1) * 64, col + par * 64:col + par * 64 + 64],
                        in1=tri[par * 64:(par + 1) * 64, par * 64:par * 64 + 64], op=ALU.mult),
                        r=[psk(abk), 'tri'], w=['AT'])
            for b3 in range(3):
                obk = Mbank()
                for tj in range(3):
                    ti = b3 * 3 + tj
                    tl = 7 + ti
                    col = tj * 128
                    S.op('pe', lambda e, obk=obk, col=col, tl=tl, ti=ti: e.matmul(
                        out=PS[obk][:, col:col + 128], lhsT=vtok[:, tl * 128:(tl + 1) * 128],
                        rhs=AT[:, ti * 128:(ti + 1) * 128], start=True, stop=False),
                        r=['vtok', 'AT'], w=[psk(obk)])
                    for par in range(2):
                        oc = 2 * tl + par
                        S.op('pe', lambda e, obk=obk, col=col, par=par, oc=oc, ti=ti: e.matmul(
                            out=PS[obk][:, col + par * 64:col + par * 64 + 64],
                            lhsT=Stil[:, (oc - 14) * 128:(oc - 13) * 128],
                            rhs=QgT[:, ti * 128 + par * 64:ti * 128 + par * 64 + 64], start=False,
                            stop=(par == 1)), r=['Stil', 'QgT'], w=[psk(obk)])
                S.op('act', lambda e, obk=obk: e.activation(out=osb[:, :], in_=PS[obk][:, :384], func=AF.Copy),
                     r=[psk(obk)], w=['B3'])
                S.op('act', lambda e, obk=obk: e.activation(out=osq[:, :], in_=PS[obk][:, :384], func=AF.Square),
                     r=[psk(obk)], w=['osq'])
                mbk = Mbank()
                S.op('pe', lambda e, mbk=mbk: e.matmul(out=PS[mbk][:, :384], lhsT=onesm[:, :], rhs=osq[:, :],
                                                       start=True, stop=True), r=['osq', 'onesm'], w=[psk(mbk)])
                S.op('act', lambda e, mbk=mbk: e.activation(out=rstd[:, :], in_=PS[mbk][:, :384], func=AF.Sqrt,
                                                            bias=epsT[:, 0:1]), r=[psk(mbk), 'epsT'], w=['B3'])
                S.op('dve', lambda e: e.reciprocal(out=rstd[:, :], in_=rstd[:, :]), r=['B3'], w=['B3'])
                S.op('dve', lambda e: e.tensor_tensor(out=osb[:, :], in0=osb[:, :], in1=rstd[:, :], op=ALU.mult),
                     r=['B3'], w=['B3'])
                S.op('dve', lambda e, h=h, b3=b3: e.scalar_tensor_tensor(
                    out=AoT[:, h * NE + b3 * 384:h * NE + (b3 + 1) * 384], in0=osb[:, :], scalar=hng[:, 0:1],
                    in1=gsT[:, b3 * 384:(b3 + 1) * 384], op0=ALU.mult, op1=ALU.mult),
                    r=['B3', 'gsT', 'hng'], w=['AoT'])
        S.barrier()
        if debug:
            dbuf = B1[:, 0:NE]
            for h in range(16):
                S.op('dve', lambda e, h=h: e.tensor_copy(out=dbuf[:, :], in_=AoT[:, h * NE:(h + 1) * NE]),
                     w=['dbuf'])
                S.dma('sp', lambda e, h=h: e.dma_start(out=dbgA[:, h * NE:(h + 1) * NE], in_=dbuf[:, :]),
                      r=['dbuf'])
            for h in range(4):
                S.op('dve', lambda e, h=h: e.tensor_copy(out=dbuf[:, :], in_=BoT[:, h * NE:(h + 1) * NE]),
                     w=['dbuf'])
                S.dma('sp', lambda e, h=h: e.dma_start(out=dbgB[:, h * NE:(h + 1) * NE], in_=dbuf[:, :]),
                      r=['dbuf'])
            S.barrier()

        maybe_stop(3)
        cv = Carver(9216)
        mixT = cv.get(16 * NE, BF16)
        rb32 = RINGB[:, 2 * 2048:6 * 2048].bitcast(F32)
        sga = rb32[:, 0:NE]; sgb = rb32[:, NE:2 * NE]; m1 = rb32[:, 2 * NE:3 * NE]
        ringB_full = ringB
        ringB = Ring('rB', ringB_full.bufs[0:2])

        def AoT_rhs(kc, t0, n):
            return AoT[:, kc * NE + t0 - E0:kc * NE + t0 - E0 + n]

        def BoT_rhs(kc, t0, n):
            return BoT[:, kc * NE + t0 - E0:kc * NE + t0 - E0 + n]

        pbslab = None
        for cc in range(16):
            s_ga = load_slab(ringA, ('ga', cc))

            def ev_ga(bi, blk, pss, pks):
                t0, n = blk
                S.op('act', lambda e: e.activation(out=sga[:, t0 - E0:t0 - E0 + n], in_=pss[0][:, :n],
                                                   func=AF.Sigmoid), r=[pks[0]], w=['sga'])
            dense_fm([s_ga], 16, uT_rhs, EBLK, ev_ga)
            s_pa = load_slab(ringA, ('pa', cc))

            def ev_pa(bi, blk, pss, pks):
                t0, n = blk
                S.op('dve', lambda e: e.tensor_tensor(out=m1[:, t0 - E0:t0 - E0 + n], in0=pss[0][:, :n],
                                                      in1=sga[:, t0 - E0:t0 - E0 + n], op=ALU.mult),
                     r=[pks[0], 'sga'], w=['m1'])
            dense_fm([s_pa], 16, AoT_rhs, EBLK, ev_pa)
            s_gb = load_slab(ringA, ('gb', cc))

            def ev_gb(bi, blk, pss, pks):
                t0, n = blk
                S.op('act', lambda e: e.activation(out=sgb[:, t0 - E0:t0 - E0 + n], in_=pss[0][:, :n],
                                                   func=AF.Sigmoid), r=[pks[0]], w=['sgb'])
            dense_fm([s_gb], 16, uT_rhs, EBLK, ev_gb)
            if cc % 4 == 0:
                pbslab = load_slab(ringB, ('pb', cc // 4))
            pbuf = pbslab[0][:, (cc % 4) * 512:(cc % 4 + 1) * 512]

            def ev_pb(bi, blk, pss, pks, cc=cc):
                t0, n = blk
                S.op('dve', lambda e: e.tensor_tensor(out=sgb[:, t0 - E0:t0 - E0 + n], in0=pss[0][:, :n],
                                                      in1=sgb[:, t0 - E0:t0 - E0 + n], op=ALU.mult),
                     r=[pks[0], 'sgb'], w=['sgb'])
                S.op('pool', lambda e: e.tensor_tensor(
                    out=mixT[:, cc * NE + t0 - E0:cc * NE + t0 - E0 + n], in0=m1[:, t0 - E0:t0 - E0 + n],
                    in1=sgb[:, t0 - E0:t0 - E0 + n], op=ALU.add), r=['m1', 'sgb'], w=['mixT'])
            dense_fm([(pbuf, pbslab[1])], 4, BoT_rhs, EBLK, ev_pb)
        S.barrier()

        maybe_stop(4)
        ringB = ringB_full
        x7 = RW[:, 0:2048]
        S.dma('sp', lambda e: e.dma_start(out=x7, in_=xc[E0:E0 + 128, :]), w=['x7'])
        for j in range(8):
            S.dma('sp', lambda e, j=j: e.dma_start(out=h1[:, j, :], in_=xc[1024 + 128 * j:1024 + 128 * (j + 1), :]),
                  w=[('h1', j)])
        for cb in range(4):
            sls = [load_slab(ringB, ('out', cb, kq)) for kq in range(4)]
            for ti in range(9):
                bk = Dbank()
                for kc in range(16):
                    buf, rk = sls[kc // 4]
                    kl = kc % 4
                    S.op('pe', lambda e, bk=bk, kc=kc, buf=buf, kl=kl, ti=ti: e.matmul(
                        out=PS[bk][:, :], lhsT=mixT[:, kc * NE + ti * 128:kc * NE + (ti + 1) * 128],
                        rhs=buf[:, kl * 512:(kl + 1) * 512], start=(kc == 0), stop=(kc == 15)),
                        r=[rk], w=[psk(bk)])
                if ti == 0:
                    tgt = x7[:, cb * 512:(cb + 1) * 512]
                    tres = 'x7'
                else:
                    tgt = h1[:, ti - 1, cb * 512:(cb + 1) * 512]
                    tres = ('h1', ti - 1)
                S.op('dve', lambda e, bk=bk, tgt=tgt: e.tensor_tensor(out=tgt, in0=PS[bk][:, :], in1=tgt, op=ALU.add),
                     r=[psk(bk), tres], w=[tres])
        S.barrier()
        if debug:
            for j in range(8):
                S.dma('sp', lambda e, j=j: e.dma_start(out=dbgH[128 * j:128 * (j + 1), :], in_=h1[:, j, :]))
            S.barrier()

        maybe_stop(5)
        cvf = Carver(2048)
        NV = 1026
        vT = cvf.get(16 * NV, BF16).rearrange("p (k t) -> p k t", k=16)
        ffn_base = cvf.off

        def p6_dst(i, half, psv, pk, gT):
            if i == 0:
                S.op('dve', lambda e: e.scalar_tensor_tensor(
                    out=vT[:, half * 8:half * 8 + 8, 0:2], in0=psv[:, :, 126:128], scalar=valid[:, 0:1],
                    in1=gT[:, half * 8:half * 8 + 8].unsqueeze(2).to_broadcast([128, 8, 2]),
                    op0=ALU.mult, op1=ALU.mult), r=[pk, 'g2T', 'valid'], w=[])
            else:
                S.op('dve', lambda e: e.tensor_tensor(
                    out=vT[:, half * 8:half * 8 + 8, 2 + (i - 1) * 128:2 + i * 128], in0=psv,
                    in1=gT[:, half * 8:half * 8 + 8].unsqueeze(2).to_broadcast([128, 8, 128]), op=ALU.mult),
                    r=[pk, 'g2T'], w=[])

        norm_transpose(lambda i: (x7 if i == 0 else h1[:, i - 1, :]), 9, g2T, p6_dst, 16, base=ffn_base)
        S.barrier()

        maybe_stop(6)
        cv = Carver(ffn_base)
        hup = [cv.get(NV, F32) for _ in range(2)]
        cA = cv.get(1024, F32); cB = cv.get(1024, F32); tg = cv.get(1024, F32)
        actb = [cv.get(4 * 1024, BF16) for _ in range(2)]
        VBLK = [(342 * i, 342) for i in range(3)]

        def vT_rhs(kc, t0, n):
            return vT[:, kc, t0:t0 + n]

        G = 4
        for grp in range(NFB // G):
            ab_ = actb[grp % 2]
            ares = ('act', grp % 2)
            dsl = []
            for fi in range(G):
                fb = grp * G + fi
                for ab in range(2):
                    su = load_slab(ringA, ('up', fb, ab))
                    hb = hup[ab]
                    hres = ('hup', ab)

                    def ev_up(bi, blk, pss, pks, hb=hb, hres=hres):
                        t0, n = blk
                        S.op('act', lambda e: e.activation(out=hb[:, t0:t0 + n], in_=pss[0][:, :n], func=AF.Copy),
                             r=[pks[0]], w=[hres])
                    dense_fm([su], 16, vT_rhs, VBLK, ev_up)
                    ci = fb * 2 + ab
                    cdst = cA if ab == 0 else cB
                    cres = 'cA' if ab == 0 else 'cB'
                    w0 = convp[:, ci * 4 + 0:ci * 4 + 1]; w1 = convp[:, ci * 4 + 1:ci * 4 + 2]
                    w2 = convp[:, ci * 4 + 2:ci * 4 + 3]; bb = convp[:, ci * 4 + 3:ci * 4 + 4]
                    S.op('dve', lambda e, hb=hb, cdst=cdst, w2=w2, bb=bb: e.tensor_scalar(
                        out=cdst[:, :], in0=hb[:, 2:1026], scalar1=w2, scalar2=bb, op0=ALU.mult, op1=ALU.add),
                        r=[hres, 'convp'], w=[cres])
                    S.op('dve', lambda e, hb=hb, cdst=cdst, w1=w1: e.scalar_tensor_tensor(
                        out=cdst[:, :], in0=hb[:, 1:1025], scalar=w1, in1=cdst[:, :], op0=ALU.mult, op1=ALU.add),
                        r=[hres, cres], w=[cres])
                    S.op('dve', lambda e, hb=hb, cdst=cdst, w0=w0: e.scalar_tensor_tensor(
                        out=cdst[:, :], in0=hb[:, 0:1024], scalar=w0, in1=cdst[:, :], op0=ALU.mult, op1=ALU.add),
                        r=[hres, cres], w=[cres])
                S.op('act', lambda e: e.activation(out=tg[:, :], in_=cA[:, :], func=AF.Square), r=['cA'], w=['tg'])
                S.op('pool', lambda e: e.tensor_scalar(out=tg[:, :], in0=tg[:, :], scalar1=0.044715, scalar2=1.0,
                                                       op0=ALU.mult, op1=ALU.add), r=['tg'], w=['tg'])
                S.op('pool', lambda e: e.tensor_tensor(out=tg[:, :], in0=tg[:, :], in1=cA[:, :], op=ALU.mult),
                     r=['tg', 'cA'], w=['tg'])
                S.op('act', lambda e: e.activation(out=tg[:, :], in_=tg[:, :], func=AF.Sigmoid,
                                                   scale=2.0 * math.sqrt(2.0 / math.pi)), r=['tg'], w=['tg'])
                S.op('pool', lambda e: e.tensor_tensor(out=tg[:, :], in0=tg[:, :], in1=cA[:, :], op=ALU.mult),
                     r=['tg', 'cA'], w=['tg'])
                S.op('pool', lambda e, fi=fi, ab_=ab_: e.tensor_tensor(
                    out=ab_[:, fi * 1024:(fi + 1) * 1024], in0=tg[:, :], in1=cB[:, :], op=ALU.mult),
                    r=['tg', 'cB'], w=[ares])
                dsl.append(load_slab(ringB, ('dn', fb)))
            for j in range(8):
                for cb in range(4):
                    bk = Mbank()
                    for fi in range(G):
                        buf, rk = dsl[fi]
                        S.op('pe', lambda e, bk=bk, fi=fi, buf=buf, j=j, cb=cb, ab_=ab_: e.matmul(
                            out=PS[bk][:, :], lhsT=ab_[:, fi * 1024 + j * 128:fi * 1024 + (j + 1) * 128],
                            rhs=buf[:, cb * 512:(cb + 1) * 512], start=(fi == 0), stop=(fi == G - 1)),
                            r=[rk, ares], w=[psk(bk)])
                    tgt = h1[:, j, cb * 512:(cb + 1) * 512]
                    S.op('dve', lambda e, bk=bk, tgt=tgt: e.tensor_tensor(out=tgt, in0=PS[bk][:, :], in1=tgt,
                                                                          op=ALU.add),
                         r=[psk(bk), ('h1', j)], w=[('h1', j)])
        S.barrier()

        maybe_stop(7)
        cv = Carver(0)
        gfb = cv.get(D, F32)
        yb = [cv.get(D, F32) for _ in range(2)]
        junk2 = cv.get(D, BF16)
        S.dma('sp', lambda e: e.dma_start(out=gfb[:, :], in_=gfb_d[:, :]), w=['gfb'])
        for j in range(8):
            col = 8 + j
            S.op('act', lambda e, j=j, col=col: e.activation(out=junk2[:, :], in_=h1[:, j, :], func=AF.Square,
                                                             accum_out=ss[:, col:col + 1]),
                 w=['junk2', ('ss2', col)])
            S.op('dve', lambda e, col=col: e.tensor_scalar(out=rs[:, col:col + 1], in0=ss[:, col:col + 1],
                                                           scalar1=1.0 / D, scalar2=EPS, op0=ALU.mult, op1=ALU.add),
                 r=[('ss2', col)], w=[('rs2', col)])
            S.op('act', lambda e, col=col: e.activation(out=rs[:, col:col + 1], in_=rs[:, col:col + 1], func=AF.Sqrt),
                 r=[('rs2', col)], w=[('rs2', col)])
            S.op('dve', lambda e, col=col: e.reciprocal(out=rs[:, col:col + 1], in_=rs[:, col:col + 1]),
                 r=[('rs2', col)], w=[('rs2', col)])
            S.op('dve', lambda e, j=j, col=col: e.scalar_tensor_tensor(
                out=yb[j % 2][:, :], in0=h1[:, j, :], scalar=rs[:, col:col + 1], in1=gfb[:, :],
                op0=ALU.mult, op1=ALU.mult), r=[('rs2', col), 'gfb'], w=[('yb', j % 2)])
            S.dma('sp', lambda e, j=j: e.dma_start(out=out_d[128 * j:128 * (j + 1), :], in_=yb[j % 2][:, :]),
                  r=[('yb', j % 2)])
        S.barrier()
        S.op('sp', lambda e: e.nop())
        S.emit(nc, st)
    except _Stop:
        pass
    assert len(mask_specs) <= NMASK - 1, len(mask_specs)
    return nc, mask_specs, NMASK


def host_prepare(inp, mask_specs, NMASK):
    SI = slab_plan()
    f32 = np.float32
    W = np.asarray(inp['w_in'][0], f32)
    wall = np.empty((NSLAB, 128, 2048), f32)

    def put(idx, Wm, c0, n=128, dst0=0, ntot=128):
        kc = Wm.shape[0] // 128
        dst = wall[idx, :, :kc * ntot].reshape(128, kc, ntot)
        dst[:, :, dst0:dst0 + n] = Wm[:, c0:c0 + n].reshape(kc, 128, n).transpose(1, 0, 2)

    offs = {'hq': 0, 'hf': 2048, 'hi': 4096, 'hg': 6144, 'aq': 8192, 'ak': 9728, 'av': 11264,
            'ga': 12800, 'gb': 14848}
    ar128 = np.arange(128)
    for nm in ('hq', 'hf', 'hi', 'hg'):
        for h in range(16):
            put(SI[(nm, h)], W, offs[nm] + h * 128)
    for nm in ('aq', 'ak'):
        for g in range(3):
            for p in range(2):
                base = offs[nm] + g * 512
                for ab in range(2):
                    for hh in range(2):
                        put(SI[(nm, g, p, ab)], W, base + (2 * p + hh) * 128 + ab * 64, n=64, dst0=hh * 64)
    for g in range(3):
        for p in range(2):
            c0 = offs['av'] + g * 512 + p * 256
            a = W[:, c0:c0 + 256].reshape(16, 128, 256)
            for half in range(2):
                wall[SI[('av', g, p, half)]].reshape(128, 8, 256)[:] = a[half * 8:(half + 1) * 8].transpose(1, 0, 2)
    for nm in ('ga', 'gb'):
        for cc in range(16):
            put(SI[(nm, cc)], W, offs[nm] + cc * 128)
    Wpa = np.asarray(inp['w_proj_a'][0], f32)
    for cc in range(16):
        put(SI[('pa', cc)], Wpa, cc * 128)
    Wpb = np.asarray(inp['w_proj_b'][0], f32)
    for q in range(4):
        for j in range(4):
            c0 = (4 * q + j) * 128
            wall[SI[('pb', q)], :, j * 512:(j + 1) * 512].reshape(128, 4, 128)[:] = \
                Wpb[:, c0:c0 + 128].reshape(4, 128, 128).transpose(1, 0, 2)
    Wo = np.asarray(inp['w_out'][0], f32)
    for cb in range(4):
        a = Wo[:, cb * 512:(cb + 1) * 512].reshape(16, 128, 512)
        for kq in range(4):
            wall[SI[('out', cb, kq)]].reshape(128, 4, 512)[:] = a[kq * 4:(kq + 1) * 4].transpose(1, 0, 2)
    Wu = np.asarray(inp['w_up'][0], f32)
    for fb in range(NFB):
        for ab in range(2):
            put(SI[('up', fb, ab)], Wu, ab * DFF + fb * 128)
    Wd = np.asarray(inp['w_down'][0], f32)
    for fb in range(NFB):
        wall[SI[('dn', fb)]] = Wd[fb * 128:(fb + 1) * 128, :]

    shared = {'wall': wall}
    shared['g1T'] = np.ascontiguousarray(np.asarray(inp['norm1_g'][0], f32).reshape(16, 128).T)
    shared['g2T'] = np.ascontiguousarray(np.asarray(inp['norm2_g'][0], f32).reshape(16, 128).T)
    shared['gfb'] = np.ascontiguousarray(np.broadcast_to(np.asarray(inp['norm_f_g'], f32)[None, :], (128, D)))
    lbp = np.asarray(inp['hg_lb_param'], f32).reshape(2, 16, 128).transpose(2, 0, 1).reshape(128, 32)
    shared['lbp'] = np.ascontiguousarray(lbp)
    shared['hng'] = np.ascontiguousarray(np.asarray(inp['hg_norm_g'][0], f32).reshape(128, 1))
    cw = np.asarray(inp['conv_w'][0], f32)
    cbias = np.asarray(inp['conv_b'][0], f32)
    convp = np.empty((128, 88, 4), f32)
    for fb in range(NFB):
        for ab in range(2):
            cols = ab * DFF + fb * 128 + ar128
            ci = fb * 2 + ab
            convp[:, ci, 0] = cw[0, cols]; convp[:, ci, 1] = cw[1, cols]; convp[:, ci, 2] = cw[2, cols]
            convp[:, ci, 3] = cbias[cols]
    shared['convp'] = convp.reshape(128, 352)
    shared['ident'] = np.eye(128, dtype=f32)

    half = 64
    inv_freq = np.exp(np.float32(-math.log(10000.0)) * np.arange(half, dtype=f32) * np.float32(2.0 / 128)).astype(f32)
    a_ = np.arange(128)[None, :]
    b_ = np.arange(128)[:, None]
    per_core = []
    x = np.asarray(inp['x'], f32)
    for c in range(8):
        b, s = c // 2, c % 2
        T0 = 1024 * s
        xcc = np.zeros((T, D), f32)
        if s == 1:
            xcc[:, :] = x[b]
        else:
            xcc[1024:, :] = x[b, :1024]
        pos = (np.arange(T) + (T0 - 1024)).astype(f32)
        ang = (pos[:, None] * inv_freq[None, :]).astype(f32)
        cosv = np.cos(ang).astype(f32).T
        sinv = np.sin(ang).astype(f32).T
        cs = np.concatenate([np.concatenate([cosv, cosv], 0), np.concatenate([sinv, sinv], 0)], axis=1)
        masks = np.zeros((128, NMASK, 128), f32)
        for i, (lo, hi, vm) in enumerate(mask_specs):
            m = ((b_ - a_) >= lo) & ((b_ - a_) <= hi)
            if vm == 1:
                m = m & bool(s)
            elif vm == 2:
                m = m & ((b_ >= 64) | bool(s))
            masks[:, i, :] = m
        tri = ((b_ % 64) <= (a_ % 64)) & ((b_ // 64) == (a_ // 64))
        masks[:, NMASK - 1, :] = tri
        d = dict(shared)
        d['xc'] = xcc
        d['cs'] = np.ascontiguousarray(cs)
        d['masks'] = masks.reshape(128, NMASK * 128)
        d['valid'] = np.full((128, 1), float(s), f32)
        per_core.append(d)
    return per_core


_CACHE = {}


def kernel(**inputs):
    if 'nc' not in _CACHE:
        _CACHE['nc'] = build_nc(False)
    nc, mask_specs, NMASK = _CACHE['nc']
    in_maps = host_prepare(inputs, mask_specs, NMASK)
    res = run_bass_kernel_spmd(nc, in_maps, core_ids=list(range(8)))
    out = np.empty((4, 2048, D), np.float32)
    for c in range(8):
        b, s = c // 2, c % 2
        out[b, 1024 * s:1024 * (s + 1), :] = res.results[c]['y']
    return out
```
